# Optimizing a Trainium2 kernel written in Bass

```python
import jax
import jax.numpy as jnp
from jax import lax
import numpy as np

D_MODEL = 2048
BATCH = 1
SEQ = 8192
DEPTH = 1

SSM_HEADS = 32
SSM_HEAD_DIM = 64
SSM_GROUPS = 8
SSM_HPG = SSM_HEADS // SSM_GROUPS
SSM_STATE = 128
SSM_INNER = SSM_HEADS * SSM_HEAD_DIM
SSM_CHUNK = 128
DN_K_HEADS = 8
DN_V_HEADS = 16
DN_HEAD_K = 128
DN_HEAD_V = 128
DN_KEY_DIM = DN_K_HEADS * DN_HEAD_K
DN_VAL_DIM = DN_V_HEADS * DN_HEAD_V
DN_CHUNK = 64
CONV_WIDTH = 5
N_DIR = 2
N_BRANCH = 2
CONV_DIM = SSM_INNER + 2 * SSM_GROUPS * SSM_STATE + 2 * DN_KEY_DIM + DN_VAL_DIM
PROJ_SIZES = (CONV_DIM, SSM_INNER, DN_VAL_DIM, N_DIR * SSM_HEADS, N_DIR * DN_V_HEADS, N_DIR * DN_V_HEADS, N_BRANCH * D_MODEL)
IN_DIM = sum(PROJ_SIZES)
CONV_SIZES = (SSM_INNER, SSM_GROUPS * SSM_STATE, SSM_GROUPS * SSM_STATE, DN_KEY_DIM, DN_KEY_DIM, DN_VAL_DIM)
DEEPNORM_ALPHA = (2 * DEPTH) ** 0.25
DEEPNORM_BETA = (8 * DEPTH) ** -0.25
RMS_EPS = 1e-6
LN_EPS = 1e-5

kernel_name = "bidir_ssd_deltanet_gated_merge_deepnorm"


def _split(t, sizes):
    idx = np.cumsum(np.array(sizes[:-1])).tolist()
    return jnp.split(t, idx, axis=-1)


def _flip(t):
    return jnp.flip(t, axis=1)


def _rmsnorm(x, w):
    xf = x.astype(jnp.float32)
    return xf * lax.rsqrt(jnp.mean(xf * xf, axis=-1, keepdims=True) + RMS_EPS) * w.astype(jnp.float32)


def _layernorm(x, g, b):
    xf = x.astype(jnp.float32)
    mu = jnp.mean(xf, axis=-1, keepdims=True)
    xc = xf - mu
    var = jnp.mean(xc * xc, axis=-1, keepdims=True)
    return xc * lax.rsqrt(var + LN_EPS) * g.astype(jnp.float32) + b.astype(jnp.float32)


def _l2norm(x):
    return x * lax.rsqrt(jnp.sum(x * x, axis=-1, keepdims=True) + 1e-6)


def _centred_dwconv(u, w, b):
    ch = u.shape[-1]
    pad = (CONV_WIDTH - 1) // 2
    y = lax.conv_general_dilated(u, w[:, None, :].astype(u.dtype), (1,), [(pad, pad)],
                                 dimension_numbers=('NWC', 'WIO', 'NWC'), feature_group_count=ch)
    return y + b.astype(u.dtype)


def _ssd_chunked(x, dt, a, bm, cm):
    bsz, s, g, hg, p = x.shape
    q = SSM_CHUNK
    n = s // q
    x = x.reshape(bsz, n, q, g, hg, p)
    dt = dt.reshape(bsz, n, q, g, hg)
    bm = bm.reshape(bsz, n, q, g, -1)
    cm = cm.reshape(bsz, n, q, g, -1)
    a_cum = jnp.cumsum(dt * a, axis=2)
    xdt = x * dt[..., None]
    tril = jnp.tril(jnp.ones((q, q), dtype=bool))
    seg = a_cum[:, :, :, None] - a_cum[:, :, None, :]
    lmat = jnp.exp(jnp.where(tril[:, :, None, None], seg, -jnp.inf))
    cb = jnp.einsum('bnlgk,bnsgk->bnlsg', cm, bm)
    y_diag = jnp.einsum('bnlsgh,bnsghp->bnlghp', cb[..., None] * lmat, xdt)
    decay_states = jnp.exp(a_cum[:, :, -1:] - a_cum)
    states = jnp.einsum('bnsgk,bnsghp->bnghpk', bm, xdt * decay_states[..., None])
    chunk_decay = jnp.exp(a_cum[:, :, -1])

    def step(h, inp):
        st, dec = inp
        return h * dec[..., None, None] + st, h

    h0 = jnp.zeros_like(states[:, 0])
    _, prev = lax.scan(step, h0, (jnp.moveaxis(states, 1, 0), jnp.moveaxis(chunk_decay, 1, 0)))
    prev = jnp.moveaxis(prev, 0, 1)
    y_off = jnp.einsum('bnlgk,bnghpk->bnlghp', cm, prev) * jnp.exp(a_cum)[..., None]
    return (y_diag + y_off).reshape(bsz, s, g, hg, p)


def _gated_delta_chunked(q, k, v, beta, g):
    bsz, s, h, dk = q.shape
    dv = v.shape[-1]
    c = DN_CHUNK
    n = s // c

    def chunks(t):
        return jnp.moveaxis(t.reshape((bsz, n, c, h) + t.shape[3:]), 3, 1)

    q, k, v, beta, g = chunks(q), chunks(k), chunks(v), chunks(beta), chunks(g)
    g_cum = jnp.cumsum(g, axis=-1)
    tril = jnp.tril(jnp.ones((c, c), dtype=bool))
    strict = jnp.tril(jnp.ones((c, c), dtype=bool), -1)
    decay = jnp.exp(jnp.where(tril, g_cum[..., :, None] - g_cum[..., None, :], -jnp.inf))
    k_beta = k * beta[..., None]
    v_beta = v * beta[..., None]
    a_mat = jnp.where(strict, jnp.einsum('bhnid,bhnjd->bhnij', k_beta, k) * decay, 0.0)
    eye = jnp.eye(c, dtype=a_mat.dtype)
    t_inv = lax.linalg.triangular_solve(eye + a_mat, jnp.broadcast_to(eye, a_mat.shape),
                                        left_side=True, lower=True)
    u = t_inv @ v_beta
    w = t_inv @ (k_beta * jnp.exp(g_cum)[..., None])
    attn = jnp.where(tril, jnp.einsum('bhnid,bhnjd->bhnij', q, k) * decay, 0.0)
    q_dec = q * jnp.exp(g_cum)[..., None]
    k_dec = k * jnp.exp(g_cum[..., -1:] - g_cum)[..., None]
    last = jnp.exp(g_cum[..., -1])

    def step(state, inp):
        u_i, w_i, q_i, k_i, attn_i, last_i = inp
        v_new = u_i - w_i @ state
        o = q_i @ state + attn_i @ v_new
        state = state * last_i[..., None, None] + jnp.einsum('bhck,bhcv->bhkv', k_i, v_new)
        return state, o

    xs = tuple(jnp.moveaxis(t, 2, 0) for t in (u, w, q_dec, k_dec, attn, last))
    state0 = jnp.zeros((bsz, h, dk, dv), dtype=u.dtype)
    _, o = lax.scan(step, state0, xs)
    o = jnp.moveaxis(o, 0, 2)
    return jnp.moveaxis(o, 1, 3).reshape(bsz, s, h, dv)


def setup_inputs(seed: int = 0) -> dict:
    key = jax.random.key(seed)
    ks = jax.random.split(key, 24)
    L, D = DEPTH, D_MODEL
    f32 = jnp.float32

    def nrm(k, shape, std):
        return jax.random.normal(k, shape, f32) * std

    def inv_softplus_dt(k, shape):
        dt = jnp.exp(jax.random.uniform(k, shape, f32, minval=np.log(1e-3), maxval=np.log(1e-1)))
        return dt + jnp.log(-jnp.expm1(-dt))

    return {
        'x': nrm(ks[0], (BATCH, SEQ, D), 1.0),
        'c': nrm(ks[1], (BATCH, D), 1.0),
        'w_ada': nrm(ks[2], (L, D, 3 * D), 0.1 * D ** -0.5),
        'b_ada': nrm(ks[3], (L, 3 * D), 0.02),
        'w_in': nrm(ks[4], (L, D, IN_DIM), D ** -0.5),
        'conv_w': nrm(ks[5], (L, CONV_WIDTH, CONV_DIM), CONV_WIDTH ** -0.5),
        'conv_b': nrm(ks[6], (L, CONV_DIM), 0.02),
        'ssm_a_log': jnp.log(jax.random.uniform(ks[7], (L, N_DIR, SSM_HEADS), f32, minval=1.0, maxval=16.0)),
        'ssm_dt_bias': inv_softplus_dt(ks[8], (L, N_DIR, SSM_HEADS)),
        'ssm_d': 1.0 + nrm(ks[9], (L, SSM_HEADS), 0.1),
        'ssm_norm_w': 1.0 + nrm(ks[10], (L, SSM_INNER), 0.02),
        'dn_a_log': jnp.log(jax.random.uniform(ks[11], (L, N_DIR, DN_V_HEADS), f32, minval=1.0, maxval=16.0)),
        'dn_dt_bias': inv_softplus_dt(ks[12], (L, N_DIR, DN_V_HEADS)),
        'dn_norm_w': 1.0 + nrm(ks[13], (L, DN_HEAD_V), 0.02),
        'w_branch_ssm': nrm(ks[14], (L, SSM_INNER, D), SSM_INNER ** -0.5 * DEEPNORM_BETA),
        'w_branch_dn': nrm(ks[15], (L, DN_VAL_DIM, D), DN_VAL_DIM ** -0.5 * DEEPNORM_BETA),
        'w_out': nrm(ks[16], (L, D, D), D ** -0.5 * DEEPNORM_BETA),
        'ln_g': 1.0 + nrm(ks[17], (L, D), 0.02),
        'ln_b': nrm(ks[18], (L, D), 0.02),
    }


def reference(x, c, w_ada, b_ada, w_in, conv_w, conv_b, ssm_a_log, ssm_dt_bias, ssm_d, ssm_norm_w,
              dn_a_log, dn_dt_bias, dn_norm_w, w_branch_ssm, w_branch_dn, w_out, ln_g, ln_b):
    f32 = jnp.float32
    bsz, s, _ = x.shape
    for l in range(DEPTH):
        shift, scale, gate = jnp.split(c @ w_ada[l] + b_ada[l], 3, axis=-1)
        h = x * (1.0 + scale[:, None]) + shift[:, None]
        proj = h @ w_in[l]
        conv_in, z_ssm, z_dn, dt_raw, a_raw, b_raw, gate_raw = _split(proj, PROJ_SIZES)
        u = jax.nn.silu(_centred_dwconv(conv_in, conv_w[l], conv_b[l])).astype(f32)
        xs_, bm, cm, qd, kd, vd = _split(u, CONV_SIZES)

        xs_ = xs_.reshape(bsz, s, SSM_GROUPS, SSM_HPG, SSM_HEAD_DIM)
        bm = bm.reshape(bsz, s, SSM_GROUPS, SSM_STATE)
        cm = cm.reshape(bsz, s, SSM_GROUPS, SSM_STATE)
        dt_all = jax.nn.softplus(dt_raw.astype(f32).reshape(bsz, s, N_DIR, SSM_HEADS) + ssm_dt_bias[l].astype(f32))
        a_all = -jnp.exp(ssm_a_log[l].astype(f32))
        dt_f = dt_all[:, :, 0].reshape(bsz, s, SSM_GROUPS, SSM_HPG)
        dt_b = dt_all[:, :, 1].reshape(bsz, s, SSM_GROUPS, SSM_HPG)
        y_f = _ssd_chunked(xs_, dt_f, a_all[0].reshape(SSM_GROUPS, SSM_HPG), bm, cm)
        y_b = _flip(_ssd_chunked(_flip(xs_), _flip(dt_b), a_all[1].reshape(SSM_GROUPS, SSM_HPG), _flip(bm), _flip(cm)))
        d_skip = ssm_d[l].astype(f32).reshape(SSM_GROUPS, SSM_HPG)[..., None]
        y_ssm = (y_f + y_b + d_skip * xs_).reshape(bsz, s, SSM_INNER)
        y_ssm = _rmsnorm(y_ssm * jax.nn.silu(z_ssm.astype(f32)), ssm_norm_w[l])

        rep = DN_V_HEADS // DN_K_HEADS
        q = _l2norm(qd.reshape(bsz, s, DN_K_HEADS, DN_HEAD_K)) * (DN_HEAD_K ** -0.5)
        k = _l2norm(kd.reshape(bsz, s, DN_K_HEADS, DN_HEAD_K))
        q = jnp.repeat(q, rep, axis=2)
        k = jnp.repeat(k, rep, axis=2)
        v = vd.reshape(bsz, s, DN_V_HEADS, DN_HEAD_V)
        beta = jax.nn.sigmoid(b_raw.astype(f32).reshape(bsz, s, N_DIR, DN_V_HEADS))
        g_log = -jnp.exp(dn_a_log[l].astype(f32)) * jax.nn.softplus(
            a_raw.astype(f32).reshape(bsz, s, N_DIR, DN_V_HEADS) + dn_dt_bias[l].astype(f32))
        o_f = _gated_delta_chunked(q, k, v, beta[:, :, 0], g_log[:, :, 0])
        o_b = _flip(_gated_delta_chunked(_flip(q), _flip(k), _flip(v), _flip(beta[:, :, 1]), _flip(g_log[:, :, 1])))
        z_dn_h = z_dn.astype(f32).reshape(bsz, s, DN_V_HEADS, DN_HEAD_V)
        y_dn = (_rmsnorm(o_f + o_b, dn_norm_w[l]) * jax.nn.silu(z_dn_h)).reshape(bsz, s, DN_VAL_DIM)

        p_ssm = y_ssm.astype(x.dtype) @ w_branch_ssm[l]
        p_dn = y_dn.astype(x.dtype) @ w_branch_dn[l]
        g_ssm, g_dn = jnp.split(jax.nn.sigmoid(gate_raw), N_BRANCH, axis=-1)
        mixed = (g_ssm * p_ssm + g_dn * p_dn) @ w_out[l]

        x = _layernorm(DEEPNORM_ALPHA * x + gate[:, None] * mixed, ln_g[l], ln_b[l]).astype(x.dtype)
    return x
```

```python
import numpy as np
import ml_dtypes
from contextlib import ExitStack
import concourse.bass as bass
import concourse.mybir as mybir
from concourse.bass_utils import run_bass_kernel_spmd

F32 = mybir.dt.float32
BF16 = mybir.dt.bfloat16
AF = mybir.ActivationFunctionType
ALU = mybir.AluOpType
AX = mybir.AxisListType

NCORE = 8
D = 2048
S = 8192
NCH = S // 128
NBLK = S // 512
TPC = S // NCORE
ALPHA = 2.0 ** 0.25
NEG = -30000.0
NDS = 8


class Em:
    def __init__(self, nc, es):
        self.nc = nc
        self.eng = {"pe": nc.tensor, "act": nc.scalar, "dve": nc.vector, "pool": nc.gpsimd, "sp": nc.sync}
        self.q = {k: [] for k in self.eng}
        self.csem = {k: es.enter_context(nc.semaphore("c_" + k)) for k in ("pe", "act", "dve", "pool")}
        self.ccnt = {k: 0 for k in self.csem}
        self.dsem = {k: [es.enter_context(nc.semaphore(f"d_{k}{i}")) for i in range(NDS)] for k in ("sp", "pool", "act")}
        self.dcnt = {k: 0 for k in self.dsem}
        self.dtok = {k: [None] * NDS for k in self.dsem}
        self.ccsems = [es.enter_context(nc.semaphore(f"ccsem{i}")) for i in range(12)]
        self.cccnt = 0
        self.lastw = {}
        self.readers = {}
        self.seen = {k: {} for k in self.eng}
        self.semobj = {}
        self.alias = {f"pA{i}": f"BANK{i}" for i in range(8)}
        for d in range(2):
            for nm, b in (("P0s", 0), ("P0cb", 0), ("P0kk", 0), ("P0qk", 0), ("P1_", 1), ("P2_", 2), ("P3a", 3), ("P3b", 3)):
                self.alias[f"{nm}{d}"] = f"BANK{4 * d + b}"

    def _deps(self, eng, reads, writes, is_pe_mm=False):
        toks = []
        for k in reads:
            w = self.lastw.get(k)
            if w is not None:
                toks.append(w)
        for k in writes:
            w = self.lastw.get(k)
            if w is not None:
                toks.append(w)
            toks.extend(self.readers.get(k, ()))
        waits = {}
        for (sid, val, src) in toks:
            if src == "pe" and eng == "pe" and is_pe_mm:
                continue
            if self.seen[eng].get(sid, 0) >= val:
                continue
            if waits.get(sid, 0) < val:
                waits[sid] = val
        for sid, val in waits.items():
            self.seen[eng][sid] = val
        return list(waits.items())

    def _commit(self, tok, reads, writes):
        for k in writes:
            self.lastw[k] = tok
            self.readers[k] = []
        for k in reads:
            self.readers.setdefault(k, []).append(tok)

    def op(self, eng, fn, reads=(), writes=()):
        al = self.alias
        pk = [al[k] for k in list(reads) + list(writes) if k in al]
        if pk:
            reads = [k for k in reads if k not in al]
            writes = [k for k in writes if k not in al] + sorted(set(pk))
        waits = self._deps(eng, reads, writes, is_pe_mm=(eng == "pe"))
        self.ccnt[eng] += 1
        sem = self.csem[eng]
        sid = id(sem)
        self.semobj[sid] = sem
        tok = (sid, self.ccnt[eng], eng)
        self.q[eng].append((waits, fn, sem, 1))
        self._commit(tok, reads, writes)

    def dma(self, qn, out, in_, reads=(), writes=(), **kw):
        i = self.dcnt[qn]
        self.dcnt[qn] += 1
        slot = i % NDS
        sem = self.dsem[qn][slot]
        sid = id(sem)
        self.semobj[sid] = sem
        waits = self._deps(qn, reads, writes)
        prev = self.dtok[qn][slot]
        if prev is not None and self.seen[qn].get(prev[0], 0) < prev[1]:
            waits.append((prev[0], prev[1]))
            self.seen[qn][prev[0]] = prev[1]
        tok = (sid, 16 * (i // NDS + 1), "dma")
        self.dtok[qn][slot] = tok
        self.q[qn].append((waits, lambda e: e.dma_start(out=out, in_=(in_(e) if callable(in_) else in_), **kw), sem, 16))
        self._commit(tok, reads, writes)

    def cc(self, fn, reads=(), writes=()):
        waits = self._deps("pool", reads, writes)
        sem = self.ccsems[self.cccnt]
        self.cccnt += 1
        sid = id(sem)
        self.semobj[sid] = sem
        tok = (sid, 1, "cc")
        self.q["pool"].append((waits, fn, sem, 1))
        self._commit(tok, reads, writes)

    def barrier(self):
        toks = []
        for k, sem in self.csem.items():
            if self.ccnt[k] > 0:
                toks.append((id(sem), self.ccnt[k]))
        for qn in self.dsem:
            for t in self.dtok[qn]:
                if t is not None:
                    toks.append((t[0], t[1]))
        for eng in self.eng:
            waits = []
            for sid, val in toks:
                if self.seen[eng].get(sid, 0) < val:
                    waits.append((sid, val))
                    self.seen[eng][sid] = val
            if waits:
                self.q[eng].append((waits, None, None, 0))

    def flush(self):
        self._pid = {}
        with self.nc.Block() as block:
            self.emit(block)
        self.q = {k: [] for k in self.eng}

    def pid(self, e):
        k = id(e)
        if k not in self._pid:
            self._pid[k] = e.partition_id()
        return self._pid[k]

    def final_wait(self, eng, keys):
        waits = self._deps(eng, keys, ())
        self.q[eng].append((waits, None, None, 0))

    def emit(self, block):
        def run(e, lst):
            for waits, fn, sem, inc in lst:
                for sid, val in waits:
                    e.wait_ge(self.semobj[sid], val)
                if fn is not None:
                    fn(e).then_inc(sem, inc)

        @block.tensor
        def _(e):
            run(e, self.q["pe"])

        @block.scalar
        def _(e):
            run(e, self.q["act"])

        @block.vector
        def _(e):
            run(e, self.q["dve"])

        @block.gpsimd
        def _(e):
            run(e, self.q["pool"])

        @block.sync
        def _(e):
            run(e, self.q["sp"])


def bc_mid(ap, n):
    return ap.unsqueeze(1).broadcast_to([ap.shape[0], n, ap.shape[1]])


def bc_last(ap, n):
    return ap.unsqueeze(2).broadcast_to([ap.shape[0], ap.shape[1], n])


def build(dbg=None):
    nc = bass.Bass("TRN2", target_bir_lowering=False)
    es = ExitStack()

    def din(name, shape, dt=F32):
        return nc.dram_tensor(name, list(shape), dt, kind="ExternalInput")

    def dscr(name, shape, dt=F32):
        if dbg is not None and dbg.get("s3only") and name in ("tokmaj", "fmaj", "ztok"):
            return nc.dram_tensor(name, list(shape), dt, kind="ExternalInput")
        if dbg is not None and name in dbg:
            return nc.dram_tensor(name, list(shape), dt, kind="ExternalOutput")
        return nc.dram_tensor(name, list(shape), dt)

    xT = din("xT", [D, S])
    xTs = din("xTs", [D, TPC])
    xtok = din("xtok", [TPC, D])
    cT = din("cT", [128, 16])
    w_ada = din("w_ada", [D, 3 * D])
    b_adaT = din("b_adaT", [128, 48])
    Wfm = din("Wfm", [D, 1024])
    Wtm = din("Wtm", [D, 528])
    convw = din("convw", [128, 8, 5])
    convb = din("convb", [128, 8])
    rows = din("rows", [128, 416])
    consts = din("consts", [128, 16, 128])
    w_gate = din("w_gate", [D, 2 * D])
    w_bs = din("w_bs", [D, D])
    w_bd = din("w_bd", [D, D])
    w_out = din("w_out", [D, D])
    lnrows = din("lnrows", [128, 2, D])
    out = nc.dram_tensor("out", [TPC, D], F32, kind="ExternalOutput")

    projT = dscr("projT", [1024, S + 4])
    ztok = dscr("ztok", [S, 528])
    tokmaj = dscr("tokmaj", [S, 768], BF16)
    fmaj = dscr("fmaj", [512, S], BF16)
    ydir = dscr("ydir", [2, S, 512])
    xbuf = dscr("xbuf", [NCORE, 512, TPC], BF16)
    ybuf = dscr("ybuf", [NCORE, NCORE * 512, TPC], BF16)
    ssqb = dscr("ssqb", [64, 128])
    ssqg = dscr("ssqg", [NCORE * 64, 128])
    adab = dscr("adab", [128, 6])
    adag = dscr("adag", [NCORE * 128, 6])

    em = Em(nc, es)

    def ps(name, shape, dt=F32):
        return es.enter_context(nc.psum_tensor(name, list(shape), dt))

    pA = [ps(f"pA{i}", [128, 512]) for i in range(8)]
    pB = [p[:].bitcast(BF16) for p in pA]

    def sbg(name, shape, dt=F32):
        return es.enter_context(nc.sbuf_tensor(name, list(shape), dt))

    IDN, TRIF, TRIB, NTRIF, NTRIB, ONES, MSL, MSU, MIL, MIU, MD16 = range(11)
    cst = sbg("cst", [128, 12, 128])
    em.dma("sp", cst[:], consts[:, 0:12, :], writes=["cst"])
    identb = sbg("identb", [128, 128], BF16)
    em.op("dve", lambda e: e.tensor_copy(out=identb[:], in_=cst[:, IDN, :]), reads=["cst"], writes=["identb"])
    rws = sbg("rws", [128, 416])
    em.dma("sp", rws[:], rows[:], writes=["rws"])
    ada = sbg("ada", [128, 48])
    sc1 = sbg("sc1", [128, 16])
    rr = [0]
    uid = [0]

    def alt():
        rr[0] += 1
        return "act" if rr[0] % 2 else "dve"

    def copy_op(eng, o, i):
        if eng == "act":
            return lambda e: e.activation(out=o, in_=i, func=AF.Copy)
        return lambda e: e.tensor_copy(out=o, in_=i)

    s3only = dbg is not None and dbg.get("s3only")
    with ExitStack() as st:
      if not s3only:
        def sb(name, shape, dt=F32, st=st):
            uid[0] += 1
            return st.enter_context(nc.sbuf_tensor(f"{name}_u{uid[0]}", list(shape), dt))
        Wfmb = sb("Wfmb", [128, 16, 1024], BF16)
        Wtmb = sb("Wtmb", [128, 16, 528], BF16)
        Wfm_v = Wfm.ap().rearrange("(k p) n -> p k n", p=128)
        Wtm_v = Wtm.ap().rearrange("(k p) n -> p k n", p=128)
        for i in range(4):
            em.dma("pool", Wfmb[:, :, i * 256:(i + 1) * 256], Wfm_v[:, :, i * 256:(i + 1) * 256], writes=[("Wfmb", i)])
        em.dma("pool", Wtmb[:, :, 0:256], Wtm_v[:, :, 0:256], writes=[("Wtmb", 4)])
        em.dma("pool", Wtmb[:, :, 256:528], Wtm_v[:, :, 256:528], writes=[("Wtmb", 5)])
        wkeys_fm = [("Wfmb", i) for i in range(4)]
        wkeys_tm = [("Wtmb", i) for i in range(4, 6)]
        cTs = sb("cTs", [128, 16])
        em.dma("sp", cTs[:], cT[:], writes=["cTs"])
        bada = sb("bada", [128, 48])
        em.dma("sp", bada[:], b_adaT[:], writes=["bada"])
        wst = [sb(f"wst{i}", [128, 16, 256]) for i in range(2)]
        w_ada_v = w_ada.ap().rearrange("(k p) n -> p k n", p=128)
        for g in range(24):
            t = wst[g % 2]
            em.dma("sp", t[:], w_ada_v[:, :, g * 256:(g + 1) * 256], writes=[f"wst{g%2}"])
            for mm in range(2):
                m = g * 2 + mm
                for k in range(16):
                    em.op("pe", lambda e, t=t, mm=mm, k=k, m=m: e.matmul(pA[0][:, m:m + 1], lhsT=t[:, k, mm * 128:(mm + 1) * 128],
                                                                         rhs=cTs[:, k:k + 1], start=(k == 0), stop=(k == 15)),
                          reads=[f"wst{g%2}", "cTs"], writes=["pA0"])
        em.op("dve", lambda e: e.tensor_tensor(out=ada[:], in0=pA[0][:, 0:48], in1=bada[:], op=ALU.add),
              reads=["pA0", "bada"], writes=["ada"])
        em.op("dve", lambda e: e.tensor_scalar(out=sc1[:], in0=ada[:, 16:32], scalar1=1.0, scalar2=None, op0=ALU.add),
              reads=["ada"], writes=["sc1"])
        if dbg is not None and "ada_o" in dbg:
            ada_o = nc.dram_tensor("ada_o", [128, 48], F32, kind="ExternalOutput")
            em.dma("sp", ada_o[:], ada[:], reads=["ada"], writes=["ada_o"])
        zt = sb("zt", [128, 8, 2])
        em.op("pool", lambda e: e.memset(zt[:], 0.0), writes=["zt"])
        projT_v = projT.ap().rearrange("(m p) t -> p m t", p=128)
        em.dma("pool", projT_v[:, :, 0:2], zt[:], reads=["zt"], writes=["projT_pad0"])
        em.dma("pool", projT_v[:, :, S + 2:S + 4], zt[:], reads=["zt"], writes=["projT_pad1"])
        xblk = sb("xblk", [128, 16, 512])
        hT = [sb(f"hT{i}", [128, 16, 512], BF16) for i in range(2)]
        stg = [sb(f"stg{i}", [128, 512]) for i in range(4)]
        ztl = [sb(f"ztl{i}", [128, 528]) for i in range(2)]
        xT_v = xT.ap().rearrange("(k p) t -> p k t", p=128)
        nst = 0
        for blk in range(NBLK):
            b = blk % 2
            t0 = blk * 512
            for kq in range(4):
                em.dma("sp", xblk[:, kq * 4:(kq + 1) * 4, :], xT_v[:, kq * 4:(kq + 1) * 4, t0:t0 + 512], writes=[("xblk", kq)])
            for k in range(16):
                eng = alt()
                if eng == "act":
                    fn = lambda e, k=k, b=b: e.activation(out=hT[b][:, k, :], in_=xblk[:, k, :], func=AF.Identity,
                                                          scale=sc1[:, k:k + 1], bias=ada[:, k:k + 1])
                else:
                    fn = lambda e, k=k, b=b: e.tensor_scalar(out=hT[b][:, k, :], in0=xblk[:, k, :], scalar1=sc1[:, k:k + 1],
                                                             scalar2=ada[:, k:k + 1], op0=ALU.mult, op1=ALU.add)
                em.op(eng, fn, reads=[("xblk", k // 4), "sc1", "ada"], writes=[(f"hT{b}", k)])
            hkeys = [(f"hT{b}", k) for k in range(16)]
            for m in range(8):
                pk = f"pA{m%2}"
                for k in range(16):
                    em.op("pe", lambda e, m=m, k=k, b=b: e.matmul(pA[m % 2][:], lhsT=Wfmb[:, k, m * 128:(m + 1) * 128], rhs=hT[b][:, k, :],
                                                                  start=(k == 0), stop=(k == 15)),
                          reads=hkeys + wkeys_fm, writes=[pk])
                si = nst % 4
                nst += 1
                eng = alt()
                em.op(eng, copy_op(eng, stg[si][:], pA[m % 2][:]), reads=[pk], writes=[f"stg{si}"])
                em.dma("pool", projT[m * 128:(m + 1) * 128, 2 + t0:2 + t0 + 512], stg[si][:], reads=[f"stg{si}"], writes=[("projT", m, blk)])
            for tt in range(4):
                pz, pss = 2 + tt % 2, 4 + tt % 2
                for k in range(16):
                    em.op("pe", lambda e, tt=tt, k=k, b=b, pz=pz: e.matmul(pA[pz][:], lhsT=hT[b][:, k, tt * 128:(tt + 1) * 128], rhs=Wtmb[:, k, 0:512],
                                                                         start=(k == 0), stop=(k == 15)),
                          reads=hkeys + wkeys_tm, writes=[f"pA{pz}"])
                    em.op("pe", lambda e, tt=tt, k=k, b=b, pss=pss: e.matmul(pA[pss][:, 0:16], lhsT=hT[b][:, k, tt * 128:(tt + 1) * 128], rhs=Wtmb[:, k, 512:528],
                                                                           start=(k == 0), stop=(k == 15)),
                          reads=hkeys + wkeys_tm, writes=[f"pA{pss}"])
                zi = tt % 2
                eng = alt()
                em.op(eng, copy_op(eng, ztl[zi][:, 0:512], pA[pz][:]), reads=[f"pA{pz}"], writes=[(f"ztl{zi}", 0)])
                eng = alt()
                em.op(eng, copy_op(eng, ztl[zi][:, 512:528], pA[pss][:, 0:16]), reads=[f"pA{pss}"], writes=[(f"ztl{zi}", 1)])
                em.dma("pool", ztok[t0 + tt * 128:t0 + (tt + 1) * 128, :], ztl[zi][:], reads=[(f"ztl{zi}", 0), (f"ztl{zi}", 1)],
                       writes=[("ztok", blk, tt)])
        em.barrier()
        em.flush()

    if dbg is not None and dbg.get("stop") == 1:
        es.close()
        return nc

    with ExitStack() as st:
      if not s3only:
        def sb(name, shape, dt=F32):
            return st.enter_context(nc.sbuf_tensor(name, list(shape), dt))
        cw = sb("cw", [128, 8, 5])
        cb = sb("cb", [128, 8])
        em.dma("sp", cw[:], convw[:], writes=["cw"])
        em.dma("sp", cb[:], convb[:], writes=["cb"])
        cin = [sb(f"cin{i}", [128, 8, 516], BF16) for i in range(2)]
        dgw = sb("dgw", [128, 40, 128], BF16)
        for m in range(8):
            for j in range(5):
                em.op("dve" if (m * 5 + j) % 2 else "pool", lambda e, m=m, j=j: e.tensor_scalar(out=dgw[:, m * 5 + j, :], in0=cst[:, IDN, :], scalar1=cw[:, m, j:j + 1], scalar2=None,
                                                                                             op0=ALU.mult), reads=["cst", "cw"], writes=[("dgw", m)])
        ub = [sb(f"ub{i}", [128, 6, 512], BF16) for i in range(2)]
        uq = sb("uq", [128, 2, 512])
        sq = sb("sq", [128, 2, 512])
        rs = sb("rs", [128, 2, 512])
        qn = [sb(f"qn{i}", [128, 2, 512], BF16) for i in range(2)]
        tks = [sb(f"tks{i}", [128, 768], BF16) for i in range(2)]
        projT_v = projT.ap().rearrange("(m p) t -> p m t", p=128)
        slot_of = {0: 0, 1: 1, 2: 2, 3: 3, 6: 4, 7: 5}
        ntk = 0

        def load_cin(blk):
            em.dma("pool", cin[blk % 2][:], projT_v[:, :, blk * 512:blk * 512 + 516],
                   reads=[("projT", m, bb) for m in range(8) for bb in (blk - 1, blk, blk + 1) if 0 <= bb < NBLK] + ["projT_pad0", "projT_pad1"],
                   writes=[f"cin{blk % 2}"])

        load_cin(0)
        for blk in range(NBLK):
            b = blk % 2
            t0 = blk * 512
            if blk + 1 < NBLK:
                load_cin(blk + 1)
            for m in range(8):
                a = pA[4 + m % 2]
                ak = f"pA{4 + m % 2}"
                for j in range(5):
                    em.op("pe", lambda e, a=a, m=m, b=b, j=j: e.matmul(a[:], lhsT=dgw[:, m * 5 + j, :], rhs=cin[b][:, m, j:j + 512], start=(j == 0), stop=(j == 4)),
                          reads=[f"cin{b}", ("dgw", m)], writes=[ak])
                if m in slot_of:
                    s_ = slot_of[m]
                    em.op("act", lambda e, a=a, s_=s_, b=b, m=m: e.activation(out=ub[b][:, s_, :], in_=a[:], func=AF.Silu, bias=cb[:, m:m + 1]), reads=[ak, "cb"], writes=[(f"ub{b}", s_)])
                else:
                    i = m - 4
                    em.op("act", lambda e, a=a, i=i, m=m: e.activation(out=uq[:, i, :], in_=a[:], func=AF.Silu, bias=cb[:, m:m + 1]), reads=[ak, "cb"], writes=[("uq", i)])
                    em.op("act", lambda e, i=i: e.activation(out=sq[:, i, :], in_=uq[:, i, :], func=AF.Square), reads=[("uq", i)], writes=[("sq", i)])
                    em.op("pe", lambda e, i=i: e.matmul(pA[i][:], lhsT=cst[:, ONES, :], rhs=sq[:, i, :], start=True, stop=True),
                          reads=[("sq", i), "cst"], writes=[f"pA{i}"])
                    scl, bia = (128.0, 128.0e-6) if i == 0 else (1.0, 1.0e-6)
                    em.op("act", lambda e, i=i, scl=scl, bia=bia: e.activation(out=rs[:, i, :], in_=pA[i][:], func=AF.Sqrt, scale=scl, bias=bia),
                          reads=[f"pA{i}"], writes=[("rs", i)])
                    em.op("dve", lambda e, i=i: e.reciprocal(out=rs[:, i, :], in_=rs[:, i, :]), reads=[("rs", i)], writes=[("rs", i)])
                    em.op("dve", lambda e, i=i, b=b: e.tensor_tensor(out=qn[b][:, i, :], in0=uq[:, i, :], in1=rs[:, i, :], op=ALU.mult),
                          reads=[("uq", i), ("rs", i)], writes=[(f"qn{b}", i)])
            for fi, (srct, key) in enumerate([(ub[b][:, 2, :], (f"ub{b}", 2)), (ub[b][:, 3, :], (f"ub{b}", 3)), (qn[b][:, 0, :], (f"qn{b}", 0)), (qn[b][:, 1, :], (f"qn{b}", 1))]):
                em.dma("pool", fmaj[fi * 128:(fi + 1) * 128, t0:t0 + 512], srct, reads=[key], writes=[("fmaj", fi, blk)])
            srcs = [(ub[b], 0, (f"ub{b}", 0)), (ub[b], 1, (f"ub{b}", 1)), (ub[b], 2, (f"ub{b}", 2)), (qn[b], 1, (f"qn{b}", 1)), (ub[b], 4, (f"ub{b}", 4)), (ub[b], 5, (f"ub{b}", 5))]
            for tt in range(4):
                pb = 2 + tt % 2
                for i, (tl, s_, key) in enumerate(srcs):
                    em.op("pe", lambda e, tl=tl, s_=s_, i=i, tt=tt, pb=pb: e.transpose(out=pB[pb][:, i * 128:(i + 1) * 128], in_=tl[:, s_, tt * 128:(tt + 1) * 128],
                                                                                   identity=identb[:]),
                          reads=[key, "identb"], writes=[f"pA{pb}"])
                ti = ntk % 2
                ntk += 1
                eng = alt()
                em.op(eng, copy_op(eng, tks[ti][:], pB[pb][:, 0:768]), reads=[f"pA{pb}"], writes=[f"tks{ti}"])
                em.dma("pool", tokmaj[t0 + tt * 128:t0 + (tt + 1) * 128, :], tks[ti][:], reads=[f"tks{ti}"], writes=[("tokmaj", blk * 4 + tt)])
        em.barrier()
        em.flush()

    if dbg is not None and dbg.get("stop") == 2:
        es.close()
        return nc

    with ExitStack() as st:
        def sb(name, shape, dt=F32):
            return st.enter_context(nc.sbuf_tensor(name, list(shape), dt))
        mb = sb("mb", [128, 4, 128], BF16)
        cm4 = sb("cm4", [128, 4, 128])
        em.dma("sp", cm4[:], consts[:, 12:16, :], writes=["cm4"])
        em.op("dve", lambda e: e.tensor_copy(out=mb[:], in_=cm4[:]), reads=["cm4"], writes=["mb"])
        nidn = sb("nidn", [128, 128])
        em.op("dve", lambda e: e.tensor_scalar(out=nidn[:], in0=cst[:, IDN, :], scalar1=-1.0, scalar2=None, op0=ALU.mult), reads=["cst"], writes=["nidn"])
        mS = [sb(f"mS{d}", [128, 4, 128], BF16) for d in range(2)]
        mAB = [sb(f"mAB{d}", [128, 4, 128], BF16) for d in range(2)]
        mC = [sb(f"mC{d}", [128, 2, 128], BF16) for d in range(2)]
        for d in range(2):
            ssd_m = MIU if d == 0 else MIL
            a_m = MSL if d == 0 else MSU
            b_m = MSU if d == 0 else MSL
            c_m = MIU if d == 0 else MIL
            em.op("pool", lambda e, d=d, ssd_m=ssd_m: e.tensor_copy(out=mS[d][:], in_=bc_mid(cst[:, ssd_m, :], 4)), reads=["cst"], writes=[f"mS{d}"])
            em.op("pool", lambda e, d=d, a_m=a_m: e.tensor_copy(out=mAB[d][:, 0:2, :], in_=bc_mid(cst[:, a_m, :], 2)), reads=["cst"], writes=[f"mAB{d}"])
            em.op("pool", lambda e, d=d, b_m=b_m: e.tensor_copy(out=mAB[d][:, 2:4, :], in_=bc_mid(cst[:, b_m, :], 2)), reads=["cst", f"mAB{d}"], writes=[f"mAB{d}"])
            em.op("pool", lambda e, d=d, c_m=c_m: e.tensor_copy(out=mC[d][:], in_=bc_mid(cst[:, c_m, :], 2)), reads=["cst"], writes=[f"mC{d}"])
        nega = sb("nega", [128, 12])
        em.op("act", lambda e: e.activation(out=nega[:], in_=rws[:, 16:28], func=AF.Exp), reads=["rws"], writes=["nega"])
        em.op("dve", lambda e: e.tensor_scalar(out=nega[:], in0=nega[:], scalar1=-1.0, scalar2=None, op0=ALU.mult), reads=["nega"], writes=["nega"])
        Hs = [sb(f"Hs{d}", [128, 256]) for d in range(2)]
        Hb = [sb(f"Hb{d}", [128, 256], BF16) for d in range(2)]
        Ss = [sb(f"Ss{d}", [128, 2, 128]) for d in range(2)]
        Sb = [sb(f"Sb{d}", [128, 2, 128], BF16) for d in range(2)]
        for d in range(2):
            em.op("pool", lambda e, d=d: e.memset(Hs[d][:], 0.0), writes=[f"Hs{d}"])
            em.op("pool", lambda e, d=d: e.memset(Hb[d][:], 0.0), writes=[f"Hb{d}"])
            em.op("pool", lambda e, d=d: e.memset(Ss[d][:], 0.0), writes=[f"Ss{d}"])
            em.op("pool", lambda e, d=d: e.memset(Sb[d][:], 0.0), writes=[f"Sb{d}"])
        NB = 4
        tk = [sb(f"tk{i}", [128, 768], BF16) for i in range(NB)]
        fm = [sb(f"fm{i}", [128, 4, 128], BF16) for i in range(NB)]
        sm = [sb(f"sm{i}", [128, 16]) for i in range(NB)]
        def dbl(name, shape, dt=F32):
            return [sb(f"{name}{i}", shape, dt) for i in range(NB)]
        t1 = dbl("t1", [128, 12]); spl = dbl("spl", [128, 12]); dg = dbl("dg", [128, 12]); e3 = dbl("e3", [128, 4]); l1 = dbl("l1", [128, 4])
        beta = dbl("beta", [128, 4]); cs = dbl("cs", [128, 24]); ecs = dbl("ecs", [128, 12]); ncs = dbl("ncs", [128, 12]); dif = dbl("dif", [128, 12])
        edif = dbl("edif", [128, 12]); dtd = dbl("dtd", [128, 12]); etot = dbl("etot", [128, 12]); negeg = dbl("negeg", [128, 12]); gcb = dbl("gcb", [128, 2])
        Es = dbl("Es", [128, 4, 128]); GT = dbl("GT", [128, 4, 128], BF16); xdt = dbl("xdt", [128, 4, 64], BF16); xdtd = dbl("xdtd", [128, 4, 64], BF16)
        ty = dbl("ty", [128, 4, 64]); yo = dbl("yo", [128, 512])
        E1 = dbl("E1", [128, 4, 128]); E2 = dbl("E2", [128, 2, 128]); NN = dbl("NN", [128, 4, 128], BF16); attnT = dbl("attnT", [128, 2, 128], BF16)
        YY = [dbl(f"YY{l}", [128, 4, 128], BF16) for l in range(4)]
        PP = [dbl(f"PP{l}", [128, 4, 128], BF16) for l in range(8)]
        VW = dbl("VW", [128, 4, 128], BF16); TpT = dbl("TpT", [128, 2, 128], BF16)
        R_sb = dbl("R_sb", [128, 2, 128]); r2 = dbl("r2", [128, 2, 128], BF16); vnb = dbl("vnb", [128, 2, 128], BF16); qss = dbl("qss", [128, 2, 128]); kd = dbl("kd", [128, 2, 128], BF16)
        fmaj_v = fmaj.ap().rearrange("(f p) t -> p f t", p=128)
        TRI = [TRIF, TRIB]
        NTRI = [NTRIF, NTRIB]
        visit = [0]

        def unit(d, s_, c, A, B):
            n = d * 2 + s_ % 2
            P0, P1, P2, P3 = pA[4 * d], pA[4 * d + 1], pA[4 * d + 2], pA[4 * d + 3]
            k0s, k0cb, k0kk, k0qk, k1, k2, k3a, k3b = f"P0s{d}", f"P0cb{d}", f"P0kk{d}", f"P0qk{d}", f"P1_{d}", f"P2_{d}", f"P3a{d}", f"P3b{d}"
            K_ = lambda s: f"{s}{n}"
            r0 = c * 128
            s0 = d * 4
            g0 = 8 + d * 2
            b0 = d * 2
            A.dma("sp", tk[n][:], tokmaj[r0:r0 + 128, :], reads=[("tokmaj", c)], writes=[K_("tk")])
            A.dma("sp", fm[n][:], fmaj_v[:, :, r0:r0 + 128], reads=[("fmaj", f, c // 4) for f in range(4)], writes=[K_("fm")])
            A.dma("sp", sm[n][:], ztok[r0:r0 + 128, 512:528], reads=[("ztok", c // 4, c % 4)], writes=[K_("sm")])
            A.op("dve", lambda e: e.tensor_tensor(out=t1[n][:], in0=sm[n][:, 0:12], in1=rws[:, 0:12], op=ALU.add), reads=[K_("sm"), "rws"], writes=[K_("t1")])
            A.op("act", lambda e: e.activation(out=t1[n][:], in_=t1[n][:], func=AF.Exp), reads=[K_("t1")], writes=[K_("t1")])
            A.op("act", lambda e: e.activation(out=spl[n][:], in_=t1[n][:], func=AF.Ln, bias=1.0), reads=[K_("t1")], writes=[K_("spl")])
            A.op("dve", lambda e: e.tensor_tensor(out=dg[n][:], in0=spl[n][:], in1=nega[:], op=ALU.mult), reads=[K_("spl"), "nega"], writes=[K_("dg")])
            A.op("act", lambda e: e.activation(out=e3[n][:], in_=sm[n][:, 12:16], func=AF.Exp, scale=-1.0), reads=[K_("sm")], writes=[K_("e3")])
            A.op("act", lambda e: e.activation(out=l1[n][:], in_=e3[n][:], func=AF.Ln, bias=1.0), reads=[K_("e3")], writes=[K_("l1")])
            A.op("dve", lambda e: e.tensor_scalar(out=beta[n][:], in0=e3[n][:], scalar1=1.0, scalar2=None, op0=ALU.add), reads=[K_("e3")], writes=[K_("beta")])
            A.op("dve", lambda e: e.reciprocal(out=beta[n][:], in_=beta[n][:]), reads=[K_("beta")], writes=[K_("beta")])
            A.op("pe", lambda e: e.matmul(P0[:, 0:12], lhsT=cst[:, TRI[d], :], rhs=dg[n][:], start=True, stop=True), reads=[K_("dg"), "cst"], writes=[k0s])
            A.op("pe", lambda e: e.matmul(P0[:, 12:24], lhsT=cst[:, ONES, :], rhs=dg[n][:], start=True, stop=True), reads=[K_("dg"), "cst"], writes=[k0s])
            A.op("dve", lambda e: e.tensor_copy(out=cs[n][:], in_=P0[:, 0:24]), reads=[k0s], writes=[K_("cs")])
            A.op("act", lambda e: e.activation(out=ecs[n][:], in_=cs[n][:, 0:12], func=AF.Exp), reads=[K_("cs")], writes=[K_("ecs")])
            A.op("dve", lambda e: e.tensor_scalar(out=ncs[n][:], in0=cs[n][:, 0:12], scalar1=-1.0, scalar2=None, op0=ALU.mult), reads=[K_("cs")], writes=[K_("ncs")])
            A.op("dve", lambda e: e.tensor_tensor(out=dif[n][:], in0=cs[n][:, 12:24], in1=cs[n][:, 0:12], op=ALU.subtract), reads=[K_("cs")], writes=[K_("dif")])
            A.op("act", lambda e: e.activation(out=edif[n][:], in_=dif[n][:], func=AF.Exp), reads=[K_("dif")], writes=[K_("edif")])
            A.op("dve", lambda e: e.tensor_tensor(out=dtd[n][:], in0=spl[n][:], in1=edif[n][:], op=ALU.mult), reads=[K_("spl"), K_("edif")], writes=[K_("dtd")])
            A.op("act", lambda e: e.activation(out=etot[n][:], in_=cs[n][:, 12:24], func=AF.Exp), reads=[K_("cs")], writes=[K_("etot")])
            A.op("dve", lambda e: e.tensor_scalar(out=negeg[n][:], in0=ecs[n][:], scalar1=-1.0, scalar2=None, op0=ALU.mult), reads=[K_("ecs")], writes=[K_("negeg")])
            A.op("dve", lambda e: e.tensor_tensor(out=gcb[n][:], in0=cs[n][:, g0:g0 + 2], in1=l1[n][:, b0:b0 + 2], op=ALU.subtract), reads=[K_("cs"), K_("l1")], writes=[K_("gcb")])

            A.op("pe", lambda e: e.matmul(P0[:, 128:256], lhsT=fm[n][:, 0, :], rhs=fm[n][:, 1, :], start=True, stop=True), reads=[K_("fm")], writes=[k0cb])
            A.op("pe", lambda e: e.matmul(P1[:], lhsT=identb[:], rhs=mS[d][:].rearrange("p a b -> p (a b)"), start=True, stop=False),
                  reads=["cst", f"mS{d}"], writes=[k1])
            for h in range(4):
                A.op("pe", lambda e, h=h: e.matmul(P1[:, h * 128:(h + 1) * 128], lhsT=dg[n][:, s0 + h:s0 + h + 1].to_broadcast([128, 128]), rhs=cst[:, TRI[d], :],
                                                    start=False, stop=(h == 3)), reads=[K_("dg"), "cst"], writes=[k1])
            for h in range(4):
                A.op("act", lambda e, h=h: e.activation(out=Es[n][:, h, :], in_=P1[:, h * 128:(h + 1) * 128], func=AF.Exp, bias=ncs[n][:, s0 + h:s0 + h + 1]),
                      reads=[k1, K_("ncs")], writes=[(K_("Es"), h)])
            A.op("dve", lambda e: e.tensor_tensor(out=GT[n][:], in0=Es[n][:], in1=bc_mid(P0[:, 128:256], 4), op=ALU.mult),
                  reads=[(K_("Es"), h) for h in range(4)] + [k0cb], writes=[K_("GT")])
            xv = tk[n][:, 0:256].rearrange("p (h q) -> p h q", h=4)
            A.op("pool", lambda e: e.tensor_tensor(out=xdt[n][:], in0=xv, in1=bc_last(spl[n][:, s0:s0 + 4], 64), op=ALU.mult), reads=[K_("tk"), K_("spl")], writes=[K_("xdt")])
            A.op("pool", lambda e: e.tensor_tensor(out=xdtd[n][:], in0=xv, in1=bc_last(dtd[n][:, s0:s0 + 4], 64), op=ALU.mult), reads=[K_("tk"), K_("dtd")], writes=[K_("xdtd")])
            A.op("pe", lambda e: e.matmul(P0[:, 256:384], lhsT=fm[n][:, 3, :], rhs=fm[n][:, 3, :], start=True, stop=True), reads=[K_("fm")], writes=[k0kk])
            A.op("pe", lambda e: e.matmul(P0[:, 384:512], lhsT=fm[n][:, 3, :], rhs=fm[n][:, 2, :], start=True, stop=True), reads=[K_("fm")], writes=[k0qk])
            A.op("pe", lambda e: e.matmul(P2[:], lhsT=identb[:], rhs=mAB[d][:].rearrange("p a b -> p (a b)"), start=True, stop=False),
                  reads=["cst", f"mAB{d}"], writes=[k2])
            for hv in range(2):
                gbc = dg[n][:, g0 + hv:g0 + hv + 1].to_broadcast([128, 128])
                lbc = l1[n][:, b0 + hv:b0 + hv + 1].to_broadcast([128, 128])
                A.op("pe", lambda e, hv=hv, gbc=gbc: e.matmul(P2[:, hv * 128:(hv + 1) * 128], lhsT=gbc, rhs=cst[:, NTRI[d], :], start=False, stop=False),
                      reads=[K_("dg"), "cst"], writes=[k2])
                A.op("pe", lambda e, hv=hv, gbc=gbc: e.matmul(P2[:, (2 + hv) * 128:(3 + hv) * 128], lhsT=gbc, rhs=cst[:, TRI[d], :], start=False, stop=False),
                      reads=[K_("dg"), "cst"], writes=[k2])
                A.op("pe", lambda e, hv=hv, lbc=lbc: e.matmul(P2[:, (2 + hv) * 128:(3 + hv) * 128], lhsT=lbc, rhs=nidn[:], start=False, stop=(hv == 1)),
                      reads=[K_("l1"), "nidn"], writes=[k2])
            A.op("pe", lambda e: e.matmul(P1[:, 0:256], lhsT=identb[:], rhs=mC[d][:].rearrange("p a b -> p (a b)"), start=True, stop=False),
                  reads=["cst", f"mC{d}"], writes=[k1])
            for hv in range(2):
                gbc = dg[n][:, g0 + hv:g0 + hv + 1].to_broadcast([128, 128])
                A.op("pe", lambda e, hv=hv, gbc=gbc: e.matmul(P1[:, hv * 128:(hv + 1) * 128], lhsT=gbc, rhs=cst[:, TRI[d], :], start=False, stop=(hv == 1)),
                      reads=[K_("dg"), "cst"], writes=[k1])
            for hv in range(2):
                A.op("act", lambda e, hv=hv: e.activation(out=E1[n][:, hv, :], in_=P2[:, hv * 128:(hv + 1) * 128], func=AF.Exp, bias=gcb[n][:, hv:hv + 1]),
                      reads=[k2, K_("gcb")], writes=[(K_("E1"), hv)])
                A.op("act", lambda e, hv=hv: e.activation(out=E1[n][:, 2 + hv, :], in_=P2[:, (2 + hv) * 128:(3 + hv) * 128], func=AF.Exp, bias=ncs[n][:, g0 + hv:g0 + hv + 1]),
                      reads=[k2, K_("ncs")], writes=[(K_("E1"), 2 + hv)])
                A.op("act", lambda e, hv=hv: e.activation(out=E2[n][:, hv, :], in_=P1[:, hv * 128:(hv + 1) * 128], func=AF.Exp, bias=ncs[n][:, g0 + hv:g0 + hv + 1]),
                      reads=[k1, K_("ncs")], writes=[(K_("E2"), hv)])
            A.op("dve", lambda e: e.scalar_tensor_tensor(out=NN[n][:], in0=E1[n][:], scalar=-1.0, in1=bc_mid(P0[:, 256:384], 4), op0=ALU.mult, op1=ALU.mult),
                  reads=[(K_("E1"), i) for i in range(4)] + [k0kk], writes=[K_("NN")])
            A.op("dve", lambda e: e.tensor_tensor(out=attnT[n][:], in0=E2[n][:], in1=bc_mid(P0[:, 384:512], 2), op=ALU.mult),
                  reads=[(K_("E2"), i) for i in range(2)] + [k0qk], writes=[K_("attnT")])
            Y0 = YY[0][n]
            A.op("pool", lambda e: e.tensor_tensor(out=Y0[:], in0=NN[n][:], in1=bc_mid(mb[:, 0, :], 4), op=ALU.mult), reads=[K_("NN"), "mb"], writes=[K_("YY0")])
            P = PP[0][n]
            A.op("pool", lambda e: e.tensor_tensor(out=P[:], in0=Y0[:], in1=bc_mid(identb[:], 4), op=ALU.add), reads=[K_("YY0"), "identb"], writes=[K_("PP0")])
            Yp, Ypk, Pp, Ppk = Y0, K_("YY0"), P, K_("PP0")
            for lv in range(1, 4):
                Yn, Ynk = YY[lv][n], K_(f"YY{lv}")
                for hv in range(2):
                    A.op("pe", lambda e, hv=hv, Yp=Yp: e.matmul(P1[:, hv * 128:(hv + 1) * 128], lhsT=Yp[:, 2 + hv, :], rhs=Yp[:, hv, :], start=True, stop=True),
                          reads=[Ypk], writes=[k1])
                    A.op("pe", lambda e, hv=hv, Yp=Yp: e.matmul(P1[:, (2 + hv) * 128:(3 + hv) * 128], lhsT=Yp[:, hv, :], rhs=Yp[:, 2 + hv, :], start=True, stop=True),
                          reads=[Ypk], writes=[k1])
                eng = alt()
                A.op(eng, copy_op(eng, Yn[:].rearrange("p a b -> p (a b)"), P1[:]), reads=[k1], writes=[Ynk])
                Pn, Pnk = PP[lv][n], K_(f"PP{lv}")
                A.op("pe", lambda e, Pp=Pp: e.matmul(P2[:], lhsT=identb[:], rhs=Pp[:].rearrange("p a b -> p (a b)"), start=True, stop=False),
                      reads=[Ppk, "identb"], writes=[k2])
                for hv in range(2):
                    A.op("pe", lambda e, hv=hv, Yn=Yn, Pp=Pp: e.matmul(P2[:, hv * 128:(hv + 1) * 128], lhsT=Yn[:, 2 + hv, :], rhs=Pp[:, hv, :], start=False, stop=False),
                          reads=[Ynk, Ppk], writes=[k2])
                    A.op("pe", lambda e, hv=hv, Yn=Yn, Pp=Pp: e.matmul(P2[:, (2 + hv) * 128:(3 + hv) * 128], lhsT=Yn[:, hv, :], rhs=Pp[:, 2 + hv, :], start=False, stop=(hv == 1)),
                          reads=[Ynk, Ppk], writes=[k2])
                eng = alt()
                A.op(eng, copy_op(eng, Pn[:].rearrange("p a b -> p (a b)"), P2[:]), reads=[k2], writes=[Pnk])
                Yp, Ypk, Pp, Ppk = Yn, Ynk, Pn, Pnk
            X, Xk = Pp, Ppk
            for li in range(3):
                last = (li == 2)
                for hv in range(2):
                    if not last:
                        A.op("pe", lambda e, hv=hv, X=X: e.matmul(P1[:, hv * 128:(hv + 1) * 128], lhsT=NN[n][:, 2 + hv, :], rhs=X[:, hv, :], start=True, stop=True),
                              reads=[K_("NN"), Xk], writes=[k1])
                    A.op("pe", lambda e, hv=hv, X=X: e.matmul(P1[:, (2 + hv) * 128:(3 + hv) * 128], lhsT=NN[n][:, hv, :], rhs=X[:, 2 + hv, :], start=True, stop=True),
                          reads=[K_("NN"), Xk], writes=[k1])
                lo = 256 if last else 0
                A.op("dve", lambda e, li=li, lo=lo: e.scalar_tensor_tensor(out=VW[n][:].rearrange("p a b -> p (a b)")[:, lo:512].rearrange("p (a b) -> p a b", b=128),
                                                                            in0=P1[:, lo:512].rearrange("p (a b) -> p a b", b=128), scalar=1.0,
                                                                            in1=bc_mid(mb[:, 1 + li, :], (512 - lo) // 128), op0=ALU.mult, op1=ALU.mult),
                      reads=[k1, "mb"], writes=[K_("VW")])
                Xn, Xnk = PP[4 + li][n], K_(f"PP{4+li}")
                if not last:
                    A.op("pe", lambda e, X=X: e.matmul(P2[:], lhsT=identb[:], rhs=X[:].rearrange("p a b -> p (a b)"), start=True, stop=False),
                          reads=[Xk, "identb"], writes=[k2])
                else:
                    A.op("pe", lambda e, X=X: e.matmul(P2[:, 256:512], lhsT=identb[:], rhs=X[:, 2:4, :].rearrange("p a b -> p (a b)"), start=True, stop=False),
                          reads=[Xk, "identb"], writes=[k2])
                for hv in range(2):
                    if not last:
                        A.op("pe", lambda e, hv=hv, X=X: e.matmul(P2[:, hv * 128:(hv + 1) * 128], lhsT=X[:, 2 + hv, :], rhs=VW[n][:, hv, :], start=False, stop=False),
                              reads=[Xk, K_("VW")], writes=[k2])
                    A.op("pe", lambda e, hv=hv, X=X: e.matmul(P2[:, (2 + hv) * 128:(3 + hv) * 128], lhsT=X[:, hv, :], rhs=VW[n][:, 2 + hv, :], start=False, stop=(hv == 1)),
                          reads=[Xk, K_("VW")], writes=[k2])
                if not last:
                    eng = alt()
                    A.op(eng, copy_op(eng, Xn[:].rearrange("p a b -> p (a b)"), P2[:]), reads=[k2], writes=[Xnk])
                    X, Xk = Xn, Xnk
                else:
                    for hv in range(2):
                        A.op("act", lambda e, hv=hv: e.activation(out=TpT[n][:, hv, :], in_=P2[:, (2 + hv) * 128:(3 + hv) * 128], func=AF.Copy,
                                                                   scale=beta[n][:, b0 + hv:b0 + hv + 1]), reads=[k2, K_("beta")], writes=[(K_("TpT"), hv)])
            for h in range(4):
                B.op("pe", lambda e, h=h: e.matmul(P3[:, h * 64:(h + 1) * 64], lhsT=GT[n][:, h, :], rhs=xdt[n][:, h, :], start=True, stop=True),
                      reads=[K_("GT"), K_("xdt")], writes=[k3a])
            B.op("pe", lambda e: e.matmul(P3[:, 256:512], lhsT=fm[n][:, 1, :], rhs=Hb[d][:], start=True, stop=True), reads=[K_("fm"), f"Hb{d}"], writes=[k3b])
            B.op("dve", lambda e: e.tensor_tensor(out=ty[n][:], in0=P3[:, 256:512].rearrange("p (h q) -> p h q", h=4), in1=bc_last(ecs[n][:, s0:s0 + 4], 64), op=ALU.mult),
                  reads=[k3b, K_("ecs")], writes=[K_("ty")])
            B.op("dve", lambda e: e.tensor_tensor(out=yo[n][:, 0:256], in0=P3[:, 0:256], in1=ty[n][:].rearrange("p h q -> p (h q)"), op=ALU.add),
                  reads=[k3a, K_("ty")], writes=[(K_("yo"), 0)])
            B.op("pe", lambda e: e.matmul(P3[:, 0:256], lhsT=tk[n][:, 256:384], rhs=xdtd[n][:].rearrange("p h q -> p (h q)"), start=True, stop=True),
                  reads=[K_("tk"), K_("xdtd")], writes=[k3a])
            B.op("pool", lambda e: e.tensor_tensor(out=Hs[d][:].rearrange("p (h q) -> p h q", h=4), in0=Hs[d][:].rearrange("p (h q) -> p h q", h=4),
                                                    in1=bc_last(etot[n][:, s0:s0 + 4], 64), op=ALU.mult), reads=[f"Hs{d}", K_("etot")], writes=[f"Hs{d}"])
            B.op("dve", lambda e: e.tensor_tensor(out=Hs[d][:], in0=Hs[d][:], in1=P3[:, 0:256], op=ALU.add), reads=[f"Hs{d}", k3a], writes=[f"Hs{d}"])
            B.op("act", lambda e: e.activation(out=Hb[d][:], in_=Hs[d][:], func=AF.Copy), reads=[f"Hs{d}"], writes=[f"Hb{d}"])

            for hv in range(2):
                B.op("pe", lambda e, hv=hv: e.matmul(P3[:, hv * 128:(hv + 1) * 128], lhsT=fm[n][:, 3, :], rhs=Sb[d][:, hv, :], start=True, stop=True),
                      reads=[K_("fm"), f"Sb{d}"], writes=[k3a])
                B.op("pe", lambda e, hv=hv: e.matmul(P3[:, (2 + hv) * 128:(3 + hv) * 128], lhsT=fm[n][:, 2, :], rhs=Sb[d][:, hv, :], start=True, stop=True),
                      reads=[K_("fm"), f"Sb{d}"], writes=[k3b])
            B.op("act", lambda e: e.activation(out=R_sb[n][:].rearrange("p a b -> p (a b)"), in_=P3[:, 0:256], func=AF.Copy), reads=[k3a], writes=[K_("R_sb")])
            for hv in range(2):
                B.op("dve", lambda e, hv=hv: e.scalar_tensor_tensor(out=r2[n][:, hv, :], in0=R_sb[n][:, hv, :], scalar=negeg[n][:, g0 + hv:g0 + hv + 1],
                                                                     in1=tk[n][:, 512 + hv * 128:640 + hv * 128], op0=ALU.mult, op1=ALU.add),
                      reads=[K_("R_sb"), K_("negeg"), K_("tk")], writes=[(K_("r2"), hv)])
                B.op("pe", lambda e, hv=hv: e.matmul(P3[:, hv * 128:(hv + 1) * 128], lhsT=TpT[n][:, hv, :], rhs=r2[n][:, hv, :], start=True, stop=True),
                      reads=[(K_("TpT"), hv), (K_("r2"), hv)], writes=[k3a])
            B.op("act", lambda e: e.activation(out=vnb[n][:].rearrange("p a b -> p (a b)"), in_=P3[:, 0:256], func=AF.Copy), reads=[k3a], writes=[K_("vnb")])
            for hv in range(2):
                B.op("act", lambda e, hv=hv: e.activation(out=qss[n][:, hv, :], in_=P3[:, (2 + hv) * 128:(3 + hv) * 128], func=AF.Copy,
                                                           scale=ecs[n][:, g0 + hv:g0 + hv + 1]), reads=[k3b, K_("ecs")], writes=[(K_("qss"), hv)])
                B.op("act", lambda e, hv=hv: e.activation(out=kd[n][:, hv, :], in_=tk[n][:, 384:512], func=AF.Copy, scale=edif[n][:, g0 + hv:g0 + hv + 1]),
                      reads=[K_("tk"), K_("edif")], writes=[(K_("kd"), hv)])
            for hv in range(2):
                B.op("pe", lambda e, hv=hv: e.matmul(P3[:, (2 + hv) * 128:(3 + hv) * 128], lhsT=attnT[n][:, hv, :], rhs=vnb[n][:, hv, :], start=True, stop=True),
                      reads=[K_("attnT"), K_("vnb")], writes=[k3b])
                B.op("pe", lambda e, hv=hv: e.matmul(P3[:, hv * 128:(hv + 1) * 128], lhsT=kd[n][:, hv, :], rhs=vnb[n][:, hv, :], start=True, stop=True),
                      reads=[(K_("kd"), hv), K_("vnb")], writes=[k3a])
            B.op("dve", lambda e: e.tensor_tensor(out=yo[n][:, 256:512], in0=qss[n][:].rearrange("p a b -> p (a b)"), in1=P3[:, 256:512], op=ALU.add),
                  reads=[(K_("qss"), 0), (K_("qss"), 1), k3b], writes=[(K_("yo"), 1)])
            for hv in range(2):
                B.op("dve", lambda e, hv=hv: e.scalar_tensor_tensor(out=Ss[d][:, hv, :], in0=Ss[d][:, hv, :], scalar=etot[n][:, g0 + hv:g0 + hv + 1],
                                                                     in1=P3[:, hv * 128:(hv + 1) * 128], op0=ALU.mult, op1=ALU.add),
                      reads=[f"Ss{d}", K_("etot"), k3a], writes=[f"Ss{d}"])
            B.op("act", lambda e: e.activation(out=Sb[d][:].rearrange("p a b -> p (a b)"), in_=Ss[d][:].rearrange("p a b -> p (a b)"), func=AF.Copy),
                  reads=[f"Ss{d}"], writes=[f"Sb{d}"])
            B.dma("pool", ydir[d, r0:r0 + 128, :], yo[n][:], reads=[(K_("yo"), 0), (K_("yo"), 1)], writes=[("ydir", d, c)])

        class Rec:
            def __init__(self):
                self.l = []

            def op(self, *a, **k):
                self.l.append(("op", a, k))

            def dma(self, *a, **k):
                self.l.append(("dma", a, k))

        def replay(lists):
            seq = []
            for li, l in enumerate(lists):
                for i, it in enumerate(l):
                    seq.append(((i + 0.5) / len(l), li, i, it))
            seq.sort(key=lambda t: (t[0], t[1]))
            for _, _, _, (kind, a, k) in seq:
                (em.op if kind == "op" else em.dma)(*a, **k)

        nsteps = NCH if dbg is None else dbg.get("nsteps", NCH)
        prevB = []
        for s_ in range(nsteps):
            recs = [(Rec(), Rec()) for _ in range(2)]
            unit(0, s_, s_, recs[0][0], recs[0][1])
            unit(1, s_, NCH - 1 - s_, recs[1][0], recs[1][1])
            replay([recs[0][0].l, recs[1][0].l] + prevB)
            prevB = [recs[0][1].l, recs[1][1].l]
        replay(prevB)
        em.barrier()
        em.flush()

    if dbg is not None and dbg.get("stop") == 3:
        es.close()
        return nc

    with ExitStack() as st:
        def sb(name, shape, dt=F32):
            return st.enter_context(nc.sbuf_tensor(name, list(shape), dt))
        NB = 2
        def dbl(name, shape, dt=F32):
            return [sb(f"{name}{i}", shape, dt) for i in range(NB)]
        yf = dbl("yf", [128, 512]); yb = dbl("yb", [128, 512]); tkx = dbl("tkx", [128, 256], BF16); zz = dbl("zz", [128, 512])
        tmpx = dbl("tmpx", [128, 4, 64]); yg = dbl("yg", [128, 256]); junk = dbl("junk", [128, 256]); ygb = dbl("ygb", [128, 512], BF16)
        dq = dbl("dq", [128, 2]); xs_ = dbl("xs", [128, 4, 128], BF16)
        ssqcol = sb("ssqcol", [128, 64])
        ssqT = sb("ssqT", [64, 128])
        for c in range(NCH):
            n = c % NB
            K_ = lambda s: f"{s}{n}"
            r0 = c * 128
            em.dma("sp", yf[n][:], ydir[0, r0:r0 + 128, :], reads=[("ydir", 0, c)], writes=[K_("yf")])
            em.dma("sp", yb[n][:], ydir[1, r0:r0 + 128, :], reads=[("ydir", 1, c)], writes=[K_("yb")])
            em.dma("sp", tkx[n][:], tokmaj[r0:r0 + 128, 0:256], reads=[("tokmaj", c)], writes=[K_("tkx")])
            em.dma("sp", zz[n][:], ztok[r0:r0 + 128, 0:512], reads=[("ztok", c // 4, c % 4)], writes=[K_("zz")])
            em.op("dve", lambda e, n=n: e.tensor_tensor(out=yf[n][:], in0=yf[n][:], in1=yb[n][:], op=ALU.add), reads=[K_("yf"), K_("yb")], writes=[K_("yf")])
            em.op("pool", lambda e, n=n: e.tensor_tensor(out=tmpx[n][:], in0=tkx[n][:].rearrange("p (h q) -> p h q", h=4), in1=bc_last(rws[:, 28:32], 64), op=ALU.mult),
                  reads=[K_("tkx"), "rws"], writes=[K_("tmpx")])
            em.op("dve", lambda e, n=n: e.tensor_tensor(out=yf[n][:, 0:256], in0=yf[n][:, 0:256], in1=tmpx[n][:].rearrange("p h q -> p (h q)"), op=ALU.add),
                  reads=[K_("yf"), K_("tmpx")], writes=[K_("yf")])
            em.op("act", lambda e, n=n: e.activation(out=zz[n][:], in_=zz[n][:], func=AF.Silu), reads=[K_("zz")], writes=[K_("zz")])
            em.op("dve", lambda e, n=n: e.tensor_tensor(out=yg[n][:], in0=yf[n][:, 0:256], in1=zz[n][:, 0:256], op=ALU.mult), reads=[K_("yf"), K_("zz")], writes=[K_("yg")])
            em.op("act", lambda e, n=n, c=c: e.activation(out=junk[n][:], in_=yg[n][:], func=AF.Square, accum_out=ssqcol[:, c:c + 1]),
                  reads=[K_("yg")], writes=[K_("junk"), ("ssqcol", c)])
            em.op("pool", lambda e, n=n: e.tensor_tensor(out=ygb[n][:, 0:256], in0=yg[n][:], in1=rws[:, 160:416], op=ALU.mult), reads=[K_("yg"), "rws"], writes=[(K_("ygb"), 0)])
            for hv in range(2):
                em.op("act", lambda e, n=n, hv=hv: e.activation(out=junk[n][:, hv * 128:(hv + 1) * 128], in_=yf[n][:, 256 + hv * 128:384 + hv * 128], func=AF.Square,
                                                                accum_out=dq[n][:, hv:hv + 1]), reads=[K_("yf")], writes=[K_("junk"), (K_("dq"), hv)])
            em.op("act", lambda e, n=n: e.activation(out=dq[n][:], in_=dq[n][:], func=AF.Sqrt, scale=1.0 / 128.0, bias=1.0e-6),
                  reads=[(K_("dq"), 0), (K_("dq"), 1)], writes=[(K_("dq"), 0), (K_("dq"), 1)])
            em.op("dve", lambda e, n=n: e.reciprocal(out=dq[n][:], in_=dq[n][:]), reads=[(K_("dq"), 0), (K_("dq"), 1)], writes=[(K_("dq"), 0), (K_("dq"), 1)])
            em.op("pool", lambda e, n=n: e.tensor_tensor(out=zz[n][:, 256:512].rearrange("p (a b) -> p a b", a=2), in0=zz[n][:, 256:512].rearrange("p (a b) -> p a b", a=2),
                                                         in1=bc_mid(rws[:, 32:160], 2), op=ALU.mult), reads=[K_("zz"), "rws"], writes=[K_("zz")])
            for hv in range(2):
                em.op("dve", lambda e, n=n, hv=hv: e.scalar_tensor_tensor(out=ygb[n][:, 256 + hv * 128:384 + hv * 128], in0=yf[n][:, 256 + hv * 128:384 + hv * 128],
                                                                          scalar=dq[n][:, hv:hv + 1], in1=zz[n][:, 256 + hv * 128:384 + hv * 128], op0=ALU.mult, op1=ALU.mult),
                      reads=[K_("yf"), (K_("dq"), 0), (K_("dq"), 1), K_("zz")], writes=[(K_("ygb"), 1 + hv)])
            pb = 2 + c % 2
            for i in range(4):
                em.op("pe", lambda e, n=n, i=i, pb=pb: e.transpose(out=pB[pb][:, i * 128:(i + 1) * 128], in_=ygb[n][:, i * 128:(i + 1) * 128], identity=identb[:]),
                      reads=[(K_("ygb"), 0), (K_("ygb"), 1), (K_("ygb"), 2), "identb"], writes=[f"pA{pb}"])
            eng = alt()
            em.op(eng, copy_op(eng, xs_[n][:].rearrange("p a b -> p (a b)"), pB[pb][:, 0:512]), reads=[f"pA{pb}"], writes=[K_("xs")])
            em.dma("pool", xbuf[c // 8].rearrange("(f p) t -> p f t", p=128)[:, :, (c % 8) * 128:(c % 8 + 1) * 128], xs_[n][:], reads=[K_("xs")], writes=[("xbuf", c)])
            if c % 8 == 7:
                g8 = c // 8
                em.cc(lambda e, g8=g8: e.collective_compute("AllGather", ALU.bypass, replica_groups=[list(range(NCORE))], ins=[xbuf[g8].opt()], outs=[ybuf[g8].opt()]),
                      reads=[("xbuf", cc_) for cc_ in range(g8 * 8, g8 * 8 + 8)], writes=[("ybuf", g8)])
        em.op("pe", lambda e: e.transpose(out=pA[0][0:64, 0:128], in_=ssqcol[:], identity=cst[:, IDN, :]), reads=[("ssqcol", c) for c in range(NCH)] + ["cst"], writes=["pA0"])
        em.op("dve", lambda e: e.tensor_copy(out=ssqT[:], in_=pA[0][0:64, 0:128]), reads=["pA0"], writes=["ssqT"])
        em.dma("pool", ssqb[:, :], ssqT[:], reads=["ssqT"], writes=["ssqb"])
        em.barrier()
        em.cc(lambda e: e.collective_compute("AllGather", ALU.bypass, replica_groups=[list(range(NCORE))], ins=[ssqb.ap().opt()], outs=[ssqg.ap().opt()]),
              reads=["ssqb"], writes=["ssqg"])
        em.barrier()
        em.flush()

    if dbg is not None and dbg.get("stop") == 4:
        es.close()
        return nc

    with ExitStack() as st5:
        def sb5(name, shape, dt=F32):
            return st5.enter_context(nc.sbuf_tensor(name, list(shape), dt))
        mergedT = sb5("mergedT", [128, 16, TPC], BF16)

        with ExitStack() as st:
            def sb(name, shape, dt=F32):
                return st.enter_context(nc.sbuf_tensor(name, list(shape), dt))
            hTs = sb("hTs", [128, 16, TPC], BF16)
            ysd = sb("ysd", [128, 32, TPC], BF16)
            rinv = sb("rinv", [128, TPC])
            wb = [[sb(f"wb{j}_{i}", [128, 16, 128], BF16) for i in range(2)] for j in range(4)]
            sg = [sb(f"sg{i}", [128, 2, 512], BF16) for i in range(2)]
            tt_ = [sb(f"tt{i}", [128, 2, 512]) for i in range(2)]
            ttk = lambda i: [(f"tt{i}", 0), (f"tt{i}", 1)]
            xTs_v = xTs.ap().rearrange("(k p) t -> p k t", p=128)
            for q16 in range(16):
                i = q16 % 2
                t = tt_[i][:].rearrange("p a b -> p (a b)").rearrange("p (k t) -> p k t", k=16)
                em.dma("sp", t, xTs_v[:, :, q16 * 64:(q16 + 1) * 64], writes=ttk(i))
                for k4 in range(4):
                    eng = alt()
                    for k in range(k4 * 4, k4 * 4 + 4):
                        if eng == "act":
                            fn = lambda e, k=k, t=t, q16=q16: e.activation(out=hTs[:, k, q16 * 64:(q16 + 1) * 64], in_=t[:, k, :], func=AF.Identity, scale=sc1[:, k:k + 1], bias=ada[:, k:k + 1])
                        else:
                            fn = lambda e, k=k, t=t, q16=q16: e.tensor_scalar(out=hTs[:, k, q16 * 64:(q16 + 1) * 64], in0=t[:, k, :], scalar1=sc1[:, k:k + 1], scalar2=ada[:, k:k + 1],
                                                                              op0=ALU.mult, op1=ALU.add)
                        em.op(eng, fn, reads=ttk(i) + ["sc1", "ada"], writes=[("hTs", q16, k)])
            hkeys = [("hTs", q16, k) for q16 in range(16) for k in range(16)]
            ybuf_v = ybuf.ap().rearrange("r (j p) t -> p j r t", p=128)
            for q in range(NCORE):
                em.dma("sp", ysd[:, q * 4:(q + 1) * 4, :].rearrange("p j (o t) -> p j o t", o=1),
                       lambda e, q=q: ybuf_v[:, q * 4:(q + 1) * 4, bass.ds(em.pid(e), 1), :], reads=[("ybuf", g) for g in range(NCORE)], writes=[("ysd", q)])
            ykeys = [("ysd", q) for q in range(NCORE)]
            ssq_v = ssqg.ap().rearrange("(q c) p -> q (c p)", q=NCORE)
            for q in range(NCORE):
                if q == 0:
                    dst, dkey = rinv[:], ["rinv"]
                else:
                    dst, dkey = tt_[q % 2][:].rearrange("p a b -> p (a b)"), ttk(q % 2)
                em.dma("sp", dst, lambda e, q=q: ssq_v[q, bass.ds(em.pid(e) * TPC, TPC)].partition_broadcast(128), reads=["ssqg"], writes=dkey)
                if q > 0:
                    em.op("pool", lambda e, dst=dst: e.tensor_tensor(out=rinv[:], in0=rinv[:], in1=dst, op=ALU.add), reads=["rinv"] + dkey, writes=["rinv"])
            em.op("act", lambda e: e.activation(out=rinv[:], in_=rinv[:], func=AF.Sqrt, scale=1.0 / 2048.0, bias=1.0e-6), reads=["rinv"], writes=["rinv"])
            em.op("dve", lambda e: e.reciprocal(out=rinv[:], in_=rinv[:]), reads=["rinv"], writes=["rinv"])
            wsrc = [w_gate.ap().rearrange("(k p) n -> p k n", p=128), w_gate.ap().rearrange("(k p) n -> p k n", p=128),
                    w_bs.ap().rearrange("(k p) n -> p k n", p=128), w_bd.ap().rearrange("(k p) n -> p k n", p=128)]
            for m in range(16):
                wi = m % 2
                for j in range(4):
                    c0 = m * 128 + (D if j == 1 else 0)
                    em.dma("pool", wb[j][wi][:], wsrc[j][:, :, c0:c0 + 128], writes=[f"wb{j}_{wi}"])
                for th in range(2):
                    tsl = slice(th * 512, (th + 1) * 512)
                    bk = [th * 4 + j for j in range(4)]
                    for j in range(4):
                        for k in range(16):
                            if j < 2:
                                rhs = hTs[:, k, tsl]
                                rk = hkeys
                            else:
                                rhs = ysd[:, (k // 2) * 4 + (j - 2) * 2 + k % 2, tsl]
                                rk = ykeys
                            em.op("pe", lambda e, j=j, k=k, rhs=rhs, wi=wi, b_=bk[j]: e.matmul(pA[b_][:], lhsT=wb[j][wi][:, k, :], rhs=rhs, start=(k == 0), stop=(k == 15)),
                                  reads=[f"wb{j}_{wi}"] + rk, writes=[f"pA{bk[j]}"])
                    s_ = sg[th]
                    u_ = tt_[th]
                    em.op("act", lambda e, s_=s_, b_=bk[0]: e.activation(out=s_[:, 0, :], in_=pA[b_][:], func=AF.Sigmoid), reads=[f"pA{bk[0]}"], writes=[(f"sg{th}", 0)])
                    em.op("act", lambda e, s_=s_, b_=bk[1]: e.activation(out=s_[:, 1, :], in_=pA[b_][:], func=AF.Sigmoid), reads=[f"pA{bk[1]}"], writes=[(f"sg{th}", 1)])
                    em.op("dve", lambda e, u_=u_, b_=bk[2], tsl=tsl: e.tensor_tensor(out=u_[:, 0, :], in0=pA[b_][:], in1=rinv[:, tsl], op=ALU.mult),
                          reads=[f"pA{bk[2]}", "rinv"], writes=[(f"tt{th}", 0)])
                    em.op("dve", lambda e, u_=u_, s_=s_, b_=bk[3]: e.tensor_tensor(out=u_[:, 1, :], in0=pA[b_][:], in1=s_[:, 1, :], op=ALU.mult),
                          reads=[f"pA{bk[3]}", (f"sg{th}", 1)], writes=[(f"tt{th}", 1)])
                    em.op("pool", lambda e, u_=u_, s_=s_: e.tensor_tensor(out=u_[:, 0, :], in0=u_[:, 0, :], in1=s_[:, 0, :], op=ALU.mult),
                          reads=[(f"tt{th}", 0), (f"sg{th}", 0)], writes=[(f"tt{th}", 0)])
                    em.op("pool", lambda e, u_=u_, m=m, tsl=tsl: e.tensor_tensor(out=mergedT[:, m, tsl], in0=u_[:, 0, :], in1=u_[:, 1, :], op=ALU.add),
                          reads=[(f"tt{th}", 0), (f"tt{th}", 1)], writes=[("mergedT", m, th)])
            em.barrier()
            em.flush()

        with ExitStack() as st:
            def sb(name, shape, dt=F32):
                return st.enter_context(nc.sbuf_tensor(name, list(shape), dt))
            wob = sb("wob", [128, 16, D], BF16)
            lnr = sb("lnr", [128, 2, D])
            gbc = sb("gbc", [128, D])
            em.dma("sp", lnr[:], lnrows[:], writes=["lnr"])
            wo_v = w_out.ap().rearrange("(k p) n -> p k n", p=128)
            for cb in range(4):
                em.dma("pool", wob[:, :, cb * 512:(cb + 1) * 512], wo_v[:, :, cb * 512:(cb + 1) * 512], writes=[("wob", cb)])
            for g4 in range(4):
                for i in range(4):
                    m = g4 * 4 + i
                    em.op("pe", lambda e, m=m, i=i, g4=g4: e.matmul(pA[g4][:, i * 128:(i + 1) * 128], lhsT=ada[:, 32 + m:33 + m].to_broadcast([128, 128]), rhs=cst[:, IDN, :],
                                                                    start=True, stop=True), reads=["ada", "cst"], writes=[f"pA{g4}"])
                eng = alt()
                em.op(eng, copy_op(eng, gbc[:, g4 * 512:(g4 + 1) * 512], pA[g4][:]), reads=[f"pA{g4}"], writes=[("gbc", g4)])
            xt = [sb(f"xt{i}", [128, D]) for i in range(2)]
            vv = [sb(f"vv{i}", [128, D]) for i in range(2)]
            jk = sb("jk", [128, D])
            st_ = [sb(f"st{i}", [128, 4]) for i in range(2)]
            mkeys = [("mergedT", m, th) for m in range(16) for th in range(2)]
            for tt in range(TPC // 128):
                n = tt % 2
                K_ = lambda s: f"{s}{n}"
                em.dma("sp", xt[n][:], xtok[tt * 128:(tt + 1) * 128, :], writes=[K_("xt")])
                for cb in range(4):
                    b_ = (tt % 2) * 4 + cb
                    for k in range(16):
                        em.op("pe", lambda e, k=k, cb=cb, b_=b_, tt=tt: e.matmul(pA[b_][:], lhsT=mergedT[:, k, tt * 128:(tt + 1) * 128], rhs=wob[:, k, cb * 512:(cb + 1) * 512],
                                                                               start=(k == 0), stop=(k == 15)), reads=mkeys + [("wob", cb)], writes=[f"pA{b_}"])
                    em.op("dve", lambda e, cb=cb, b_=b_, n=n: e.tensor_tensor(out=vv[n][:, cb * 512:(cb + 1) * 512], in0=pA[b_][:], in1=gbc[:, cb * 512:(cb + 1) * 512], op=ALU.mult),
                          reads=[f"pA{b_}", ("gbc", cb)], writes=[(K_("vv"), cb)])
                vk = [(K_("vv"), cb) for cb in range(4)]
                em.op("dve", lambda e, n=n: e.scalar_tensor_tensor(out=vv[n][:], in0=xt[n][:], scalar=ALPHA, in1=vv[n][:], op0=ALU.mult, op1=ALU.add),
                      reads=vk + [K_("xt")], writes=vk)
                em.op("act", lambda e, n=n: e.activation(out=jk[:], in_=vv[n][:], func=AF.Copy, accum_out=st_[n][:, 0:1]), reads=vk, writes=["jk", (K_("st"), 0)])
                em.op("dve", lambda e, n=n: e.tensor_scalar(out=st_[n][:, 1:2], in0=st_[n][:, 0:1], scalar1=-1.0 / D, scalar2=None, op0=ALU.mult),
                      reads=[(K_("st"), 0)], writes=[(K_("st"), 1)])
                em.op("act", lambda e, n=n: e.activation(out=jk[:], in_=vv[n][:], func=AF.Square, bias=st_[n][:, 1:2], accum_out=st_[n][:, 2:3]),
                      reads=vk + [(K_("st"), 1)], writes=["jk", (K_("st"), 2)])
                em.op("act", lambda e, n=n: e.activation(out=st_[n][:, 3:4], in_=st_[n][:, 2:3], func=AF.Sqrt, scale=1.0 / D, bias=1.0e-5),
                      reads=[(K_("st"), 2)], writes=[(K_("st"), 3)])
                em.op("dve", lambda e, n=n: e.reciprocal(out=st_[n][:, 3:4], in_=st_[n][:, 3:4]), reads=[(K_("st"), 3)], writes=[(K_("st"), 3)])
                em.op("dve", lambda e, n=n: e.tensor_scalar(out=vv[n][:], in0=vv[n][:], scalar1=st_[n][:, 1:2], scalar2=st_[n][:, 3:4], op0=ALU.add, op1=ALU.mult),
                      reads=vk + [(K_("st"), 1), (K_("st"), 3)], writes=vk)
                em.op("pool", lambda e, n=n: e.tensor_tensor(out=vv[n][:], in0=vv[n][:], in1=lnr[:, 0, :], op=ALU.mult), reads=vk + ["lnr"], writes=vk)
                em.op("pool", lambda e, n=n: e.tensor_tensor(out=vv[n][:], in0=vv[n][:], in1=lnr[:, 1, :], op=ALU.add), reads=vk + ["lnr"], writes=vk)
                em.dma("pool", out[tt * 128:(tt + 1) * 128, :], vv[n][:], reads=vk, writes=[("out", tt)])
            em.barrier()
            em.flush()

    es.close()
    return nc


def host_consts():
    c = np.zeros((128, 16, 128), np.float32)
    i = np.arange(128)[:, None]
    j = np.arange(128)[None, :]
    c[:, 0] = (i == j)
    c[:, 1] = (i <= j)
    c[:, 2] = (i >= j)
    c[:, 3] = -c[:, 1]
    c[:, 4] = -c[:, 2]
    c[:, 5] = 1.0
    c[:, 6] = np.where(i > j, 0.0, NEG)
    c[:, 7] = np.where(i < j, 0.0, NEG)
    c[:, 8] = np.where(i >= j, 0.0, NEG)
    c[:, 9] = np.where(i <= j, 0.0, NEG)
    c[:, 12] = (i // 16 == j // 16)
    for li, b in enumerate((16, 32, 64)):
        c[:, 13 + li] = (i // (2 * b) == j // (2 * b)) & (i // b != j // b)
    return c


def _core_cols(r):
    cs = 0
    xs = list(range(cs + r * 256, cs + (r + 1) * 256)); cs += 2048
    B = list(range(cs + r * 128, cs + (r + 1) * 128)); cs += 1024
    C = list(range(cs + r * 128, cs + (r + 1) * 128)); cs += 1024
    q = list(range(cs + r * 128, cs + (r + 1) * 128)); cs += 1024
    k = list(range(cs + r * 128, cs + (r + 1) * 128)); cs += 1024
    v = list(range(cs + r * 256, cs + (r + 1) * 256)); cs += 2048
    conv = xs + B + C + q + k + v
    zs = list(range(cs + r * 256, cs + (r + 1) * 256)); cs += 2048
    zd = list(range(cs + r * 256, cs + (r + 1) * 256)); cs += 2048
    dt = [cs + d * 32 + r * 4 + h for d in range(2) for h in range(4)]; cs += 64
    a = [cs + d * 16 + r * 2 + h for d in range(2) for h in range(2)]; cs += 32
    b = [cs + d * 16 + r * 2 + h for d in range(2) for h in range(2)]; cs += 32
    return conv, zs + zd + dt + a + b, cs


def make_in_maps(inp):
    f32 = lambda a: np.ascontiguousarray(a, dtype=np.float32)
    x = np.asarray(inp["x"], np.float32)[0]
    c = np.asarray(inp["c"], np.float32)[0]
    w_in = np.asarray(inp["w_in"], np.float32)[0]
    conv_w = np.asarray(inp["conv_w"], np.float32)[0]
    conv_b = np.asarray(inp["conv_b"], np.float32)[0]
    xT = f32(x.T)
    w_ada_full = np.asarray(inp["w_ada"], np.float32)[0]
    b_ada_full = np.asarray(inp["b_ada"], np.float32)[0]
    shared = dict(
        xT=xT, cT=f32(c.reshape(16, 128).T), w_ada=f32(w_ada_full), b_adaT=f32(b_ada_full.reshape(48, 128).T),
        consts=host_consts(), w_bs=f32(inp["w_branch_ssm"][0]), w_bd=f32(inp["w_branch_dn"][0]), w_out=f32(inp["w_out"][0]),
        lnrows=f32(np.broadcast_to(np.stack([np.asarray(inp["ln_g"], np.float32)[0], np.asarray(inp["ln_b"], np.float32)[0]])[None], (128, 2, D))),
    )
    gate0 = _core_cols(0)[2]
    shared["w_gate"] = f32(w_in[:, gate0:gate0 + 2 * D])
    maps = []
    for r in range(NCORE):
        conv, tm, _ = _core_cols(r)
        rows = np.zeros((416,), np.float32)
        rows[0:8] = np.asarray(inp["ssm_dt_bias"], np.float32)[0][:, r * 4:(r + 1) * 4].reshape(-1)
        rows[8:12] = np.asarray(inp["dn_dt_bias"], np.float32)[0][:, r * 2:(r + 1) * 2].reshape(-1)
        rows[16:24] = np.asarray(inp["ssm_a_log"], np.float32)[0][:, r * 4:(r + 1) * 4].reshape(-1)
        rows[24:28] = np.asarray(inp["dn_a_log"], np.float32)[0][:, r * 2:(r + 1) * 2].reshape(-1)
        rows[28:32] = np.asarray(inp["ssm_d"], np.float32)[0][r * 4:(r + 1) * 4]
        rows[32:160] = np.asarray(inp["dn_norm_w"], np.float32)[0]
        rows[160:416] = np.asarray(inp["ssm_norm_w"], np.float32)[0][r * 256:(r + 1) * 256]
        m = dict(shared)
        m.update(

            xTs=f32(xT[:, r * TPC:(r + 1) * TPC]), xtok=f32(x[r * TPC:(r + 1) * TPC]),
            Wfm=f32(w_in[:, conv]), Wtm=f32(w_in[:, tm]),
            convw=f32(conv_w[:, conv].T.reshape(8, 128, 5).transpose(1, 0, 2)), convb=f32(conv_b[conv].reshape(8, 128).T),
            rows=f32(np.broadcast_to(rows[None], (128, 416))),
        )
        maps.append(m)
    return maps


_NC_CACHE = {}


def kernel(**inp):
    if "nc" not in _NC_CACHE:
        _NC_CACHE["nc"] = build()
    nc = _NC_CACHE["nc"]
    maps = make_in_maps(inp)
    res = run_bass_kernel_spmd(nc, maps, core_ids=list(range(NCORE)))
    outs = [np.asarray(res.results[r]["out"], dtype=np.float32) for r in range(NCORE)]
    return np.concatenate(outs, axis=0)[None].astype(np.float32)
```
